# Optimizing a Trainium2 kernel written in Bass

```python
import jax, jax.numpy as jnp
from jax import lax
import numpy as np

D_MODEL = 1024
BATCH = 8
SEQ = 2048
DEPTH = 4
DEC_BATCH = 128
DEC_SEQ = 4
PAST_LEN = 16384
PAGE_SIZE = 128

D_INNER = 2 * D_MODEL
LRU_BLOCK = 128
LRU_HEADS = D_INNER // LRU_BLOCK
LRU_CONV_W = 4
LRU_C = 8.0
CCM_CONV_W = 31
N_LRU = (DEPTH + 1) // 2
N_CCM = DEPTH // 2
RMS_EPS = 1e-6
LN_EPS = 1e-5

kernel_name = 'hawk_conformer_hybrid_step'


def rmsnorm(x, g):
    xf = x.astype(jnp.float32)
    y = xf * lax.rsqrt(jnp.mean(xf * xf, axis=-1, keepdims=True) + RMS_EPS)
    return (y * g.astype(jnp.float32)).astype(x.dtype)


def layernorm(x, g, b):
    xf = x.astype(jnp.float32)
    mu = jnp.mean(xf, axis=-1, keepdims=True)
    xc = xf - mu
    var = jnp.mean(xc * xc, axis=-1, keepdims=True)
    y = xc * lax.rsqrt(var + LN_EPS) * g.astype(jnp.float32) + b.astype(jnp.float32)
    return y.astype(x.dtype)


def causal_dwconv(x, buf, w, b):
    C = x.shape[-1]
    W = w.shape[0]
    xp = jnp.concatenate([buf.astype(x.dtype), x], axis=1)
    y = lax.conv_general_dilated(
        xp, w.astype(x.dtype)[:, None, :], window_strides=(1,), padding='VALID',
        dimension_numbers=('NWC', 'WIO', 'NWC'), feature_group_count=C)
    new_buf = xp[:, xp.shape[1] - (W - 1):]
    return y + b.astype(x.dtype), new_buf


def rg_lru(x, h0, w_a, b_a, w_i, b_i, lam):
    B, T, C = x.shape
    xf = x.astype(jnp.float32)
    xh = xf.reshape(B, T, LRU_HEADS, LRU_BLOCK)
    r = jax.nn.sigmoid(jnp.einsum('bthi,hij->bthj', xh, w_a.astype(jnp.float32)).reshape(B, T, C)
                       + b_a.astype(jnp.float32))
    i = jax.nn.sigmoid(jnp.einsum('bthi,hij->bthj', xh, w_i.astype(jnp.float32)).reshape(B, T, C)
                       + b_i.astype(jnp.float32))
    log_a = -LRU_C * r * jax.nn.softplus(-lam.astype(jnp.float32))
    a = jnp.exp(log_a)
    b = jnp.sqrt(-jnp.expm1(2.0 * log_a)) * (i * xf)
    b = b.at[:, 0].add(a[:, 0] * h0.astype(jnp.float32))

    def combine(left, right):
        a1, b1 = left
        a2, b2 = right
        return a1 * a2, a2 * b1 + b2

    _, h = lax.associative_scan(combine, (a, b), axis=1)
    return h.astype(x.dtype), h[:, -1].astype(x.dtype)


def lru_layer(x, conv_buf, h0, g, w_in, conv_w, conv_b, w_a, b_a, w_i, b_i, lam, w_out):
    hn = rmsnorm(x, g)
    u = hn @ w_in.astype(x.dtype)
    xb, z = u[..., :D_INNER], u[..., D_INNER:]
    xb, new_buf = causal_dwconv(xb, conv_buf, conv_w, conv_b)
    y, h_last = rg_lru(xb, h0, w_a, b_a, w_i, b_i, lam)
    out = (y * jax.nn.silu(z)) @ w_out.astype(x.dtype)
    return x + out, new_buf, h_last


def ccm_layer(x, conv_buf, g, w_in, b_in, dw_w, dw_b, ln_g, ln_b, w_out, b_out):
    hn = rmsnorm(x, g)
    u = hn @ w_in.astype(x.dtype) + b_in.astype(x.dtype)
    v = u[..., :D_INNER]
    gl = u[..., D_INNER:2 * D_INNER]
    z = u[..., 2 * D_INNER:]
    c = v * jax.nn.sigmoid(gl)
    c, new_buf = causal_dwconv(c, conv_buf, dw_w, dw_b)
    c = jax.nn.silu(layernorm(c, ln_g, ln_b))
    out = (c * jax.nn.silu(z)) @ w_out.astype(x.dtype) + b_out.astype(x.dtype)
    return x + out, new_buf


def setup_inputs(seed: int = 0) -> dict:
    key = jax.random.key(seed)
    ks = jax.random.split(key, 32)
    f32 = jnp.float32

    def nrm(k, shape, s):
        return jax.random.normal(k, shape, f32) * s

    u = jax.random.uniform(ks[9], (N_LRU, D_INNER), f32, minval=0.9, maxval=0.999)
    return {
        'x_prompt': nrm(ks[0], (BATCH, SEQ, D_MODEL), 1.0),
        'x_sample': nrm(ks[1], (DEC_BATCH, DEC_SEQ, D_MODEL), 1.0),
        'state_lru_conv': nrm(ks[2], (N_LRU, DEC_BATCH, LRU_CONV_W - 1, D_INNER), 1.0),
        'state_lru_h': nrm(ks[3], (N_LRU, DEC_BATCH, D_INNER), 0.5),
        'state_ccm_conv': nrm(ks[4], (N_CCM, DEC_BATCH, CCM_CONV_W - 1, D_INNER), 0.5),
        'norm_g': 1.0 + nrm(ks[5], (DEPTH, D_MODEL), 0.02),
        'final_norm_g': 1.0 + nrm(ks[6], (D_MODEL,), 0.02),
        'lru_w_in': nrm(ks[7], (N_LRU, D_MODEL, 2 * D_INNER), D_MODEL ** -0.5),
        'lru_conv_w': nrm(ks[8], (N_LRU, LRU_CONV_W, D_INNER), LRU_CONV_W ** -0.5),
        'lru_conv_b': nrm(ks[10], (N_LRU, D_INNER), 0.02),
        'lru_w_a': nrm(ks[11], (N_LRU, LRU_HEADS, LRU_BLOCK, LRU_BLOCK), LRU_BLOCK ** -0.5),
        'lru_b_a': nrm(ks[12], (N_LRU, D_INNER), 0.02),
        'lru_w_i': nrm(ks[13], (N_LRU, LRU_HEADS, LRU_BLOCK, LRU_BLOCK), LRU_BLOCK ** -0.5),
        'lru_b_i': nrm(ks[14], (N_LRU, D_INNER), 0.02),
        'lru_lam': jnp.log(u) - jnp.log1p(-u),
        'lru_w_out': nrm(ks[15], (N_LRU, D_INNER, D_MODEL), D_INNER ** -0.5),
        'ccm_w_in': nrm(ks[16], (N_CCM, D_MODEL, 3 * D_INNER), D_MODEL ** -0.5),
        'ccm_b_in': nrm(ks[17], (N_CCM, 3 * D_INNER), 0.02),
        'ccm_dw_w': nrm(ks[18], (N_CCM, CCM_CONV_W, D_INNER), CCM_CONV_W ** -0.5),
        'ccm_dw_b': nrm(ks[19], (N_CCM, D_INNER), 0.02),
        'ccm_ln_g': 1.0 + nrm(ks[20], (N_CCM, D_INNER), 0.02),
        'ccm_ln_b': nrm(ks[21], (N_CCM, D_INNER), 0.02),
        'ccm_w_out': nrm(ks[22], (N_CCM, D_INNER, D_MODEL), D_INNER ** -0.5),
        'ccm_b_out': nrm(ks[23], (N_CCM, D_MODEL), 0.02),
    }


def reference(x_prompt, x_sample, state_lru_conv, state_lru_h, state_ccm_conv,
              norm_g, final_norm_g,
              lru_w_in, lru_conv_w, lru_conv_b, lru_w_a, lru_b_a, lru_w_i, lru_b_i, lru_lam, lru_w_out,
              ccm_w_in, ccm_b_in, ccm_dw_w, ccm_dw_b, ccm_ln_g, ccm_ln_b, ccm_w_out, ccm_b_out):
    xp, xs = x_prompt, x_sample
    Bp = x_prompt.shape[0]
    dt = x_prompt.dtype
    lru_conv_p, lru_h_p, ccm_conv_p = [], [], []
    lru_conv_s, lru_h_s, ccm_conv_s = [], [], []
    for l in range(DEPTH):
        j = l // 2
        if l % 2 == 0:
            lw = (norm_g[l], lru_w_in[j], lru_conv_w[j], lru_conv_b[j], lru_w_a[j], lru_b_a[j],
                  lru_w_i[j], lru_b_i[j], lru_lam[j], lru_w_out[j])
            zc = jnp.zeros((Bp, LRU_CONV_W - 1, D_INNER), dt)
            zh = jnp.zeros((Bp, D_INNER), dt)
            xp, cb, hl = lru_layer(xp, zc, zh, *lw)
            lru_conv_p.append(cb)
            lru_h_p.append(hl)
            xs, cb, hl = lru_layer(xs, state_lru_conv[j], state_lru_h[j], *lw)
            lru_conv_s.append(cb)
            lru_h_s.append(hl)
        else:
            cw = (norm_g[l], ccm_w_in[j], ccm_b_in[j], ccm_dw_w[j], ccm_dw_b[j],
                  ccm_ln_g[j], ccm_ln_b[j], ccm_w_out[j], ccm_b_out[j])
            zc = jnp.zeros((Bp, CCM_CONV_W - 1, D_INNER), dt)
            xp, cb = ccm_layer(xp, zc, *cw)
            ccm_conv_p.append(cb)
            xs, cb = ccm_layer(xs, state_ccm_conv[j], *cw)
            ccm_conv_s.append(cb)
    y_prompt = rmsnorm(xp, final_norm_g)
    y_sample = rmsnorm(xs, final_norm_g)
    return (y_prompt, y_sample,
            jnp.stack(lru_conv_p), jnp.stack(lru_h_p), jnp.stack(ccm_conv_p),
            jnp.stack(lru_conv_s), jnp.stack(lru_h_s), jnp.stack(ccm_conv_s))
```

```python
import numpy as np
from contextlib import ExitStack
import concourse.bass as bass
import concourse.mybir as mybir
from concourse.bass_utils import run_bass_kernel_spmd

F32 = mybir.dt.float32
F32R = mybir.dt.float32r
AF = mybir.ActivationFunctionType
ALU = mybir.AluOpType

NCORES = 8
TW = 352
SEGW = 704
NSEG = 3
TALL = 2112
NCH = 16
KC = 8
NP2 = 640
P2T1 = 288
WL = 4
WC = 31
NWIN = 3
WINSZ = 2304
NWK = 13
NQ = 4
NXC = 2
TDS = (7, 7, 3)


class Sched:
    def __init__(self):
        self.prog = {e: [] for e in ("pe", "act", "dve", "pool", "sp")}
        self.cnt = {}
        self.waited = {e: {} for e in self.prog}
        self.last_w = {}
        self.readers = {}
        self.nbank = 8
        self.bank_i = 0
        self.wk_i = 0
        self.q_i = 0
        self.gen = {}

    def _keys(self, items):
        out = []
        for it in items:
            if isinstance(it, str):
                out.append(it)
            else:
                assert self.gen[it.key] == it.gen, f"stale buffer {it.key}"
                out.append(it.key)
        return out

    def op(self, eng, fn, reads=(), writes=(), sem=None, inc=1):
        semkey = sem if sem is not None else eng
        reads = self._keys(reads)
        writes = self._keys(writes)
        deps = {}

        def add(d, kind):
            if d is None:
                return
            sk, v = d
            if sk == eng and kind != "raw":
                return
            if sk == "pe" and eng == "pe":
                return
            if deps.get(sk, 0) < v:
                deps[sk] = v

        for k in reads:
            add(self.last_w.get(k), "raw")
        for k in writes:
            add(self.last_w.get(k), "waw")
            for r in self.readers.get(k, ()):
                add(r, "war")
        for sk, v in deps.items():
            if self.waited[eng].get(sk, 0) < v:
                self.prog[eng].append(("wait", sk, v))
                self.waited[eng][sk] = v
        n = self.cnt.get(semkey, 0) + inc
        self.cnt[semkey] = n
        self.prog[eng].append(("op", fn, semkey, inc))
        for k in writes:
            self.last_w[k] = (semkey, n)
            self.readers[k] = []
        for k in reads:
            self.readers.setdefault(k, []).append((semkey, n))

    def bank(self):
        b = self.bank_i % self.nbank
        self.bank_i += 1
        return b


def build_program():
    nc = bass.Bass("TRN2", target_bir_lowering=False)
    nc.dge_precook = False

    def din(name, shape, dt=F32):
        return nc.dram_tensor(name, list(shape), dt, kind="ExternalInput").ap()

    def dout(name, shape):
        return nc.dram_tensor(name, list(shape), F32, kind="ExternalOutput").ap()

    xin = din("xin", [128, KC, TALL])
    st_lc = din("st_lc", [2, 128, NCH, 16, 3])
    st_lh = din("st_lh", [128, 2, NCH, 16])
    st_cc = din("st_cc", [2, 128, NCH, 16, 30])
    w_lru = din("w_lru", [2, NCH, 128, WINSZ], F32R)
    w_cvg = din("w_cvg", [2, NCH, 128, 2048], F32R)
    w_cz = din("w_cz", [2, NCH, 128, 1024], F32R)
    w_lo = din("w_lo", [2, 8, 128, 2048], F32R)
    w_co = din("w_co", [2, 8, 128, 2048], F32R)
    pvl_d = din("pv_lru", [128, 2, NCH, 8])
    pvc_d = din("pv_ccm", [128, 2, NCH, 37])
    pm_d = din("pm", [128, KC, 7])
    ident_d = din("ident", [128, 128])

    y_d = dout("y", [128, KC, TALL])
    o_lcp = dout("o_lcp", [2, 128, NCH, 3])
    o_lhp = dout("o_lhp", [128, 2, NCH])
    o_ccp = dout("o_ccp", [2, 128, NCH, 30])
    o_lcs = dout("o_lcs", [2, 128, NCH, 16, 3])
    o_lhs = dout("o_lhs", [128, 2, NCH, 16])
    o_ccs = dout("o_ccs", [2, 128, NCH, 16, 30])

    S = Sched()
    with ExitStack() as es:
        def sb(name, shape, dt=F32):
            name = "s_" + name
            return es.enter_context(nc.sbuf_tensor(name, list(shape), dt))

        x_sb = sb("x_sb", [128, KC, SEGW])
        hn_sb = sb("hn_sb", [128, KC, SEGW])
        stash = sb("stash", [128, NCH, SEGW])
        win = [sb(f"win{i}", [128, WINSZ]) for i in range(NWIN)]
        diag = sb("diag", [128, 32, 128])
        diag_flat = diag[:].rearrange("p k n -> p (k n)")
        wout = [diag_flat[:, 2048:4096], diag_flat[:, 0:2048]]
        WOUT_KEYS = [["diag:3"], ["diag:0", "diag:1", "diag:2"]]
        DIAG_ALL = ["diag:0", "diag:1", "diag:2", "diag:3"]
        strip = [sb(f"strip{i}", [128, SEGW + WC - 1]) for i in range(2)]
        sstrip = [sb(f"sstrip{i}", [128, 16, 34]) for i in range(2)]
        wkb = [sb(f"wk{i}", [128, TW]) for i in range(NWK)]
        qb = [sb(f"q{i}", [128, TW]) for i in range(NQ)]
        xcb = [sb(f"xc{i}", [128, TW]) for i in range(NXC)]
        pvl = sb("pvl", [128, 2, NCH, 8])
        pvc = sb("pvc", [128, 2, NCH, 37])
        pm = sb("pm", [128, KC, 7])
        ident = sb("ident", [128, 128])
        ones = sb("ones", [128, 128])
        ones_f = sb("ones_f", [128, 128])
        h0s = sb("h0s", [128, 2, NCH, 16])
        der = sb("der", [128, 2, NCH, 4])
        dtmp = sb("dtmp", [128, 2, NCH])
        hist_l = sb("hist_l", [128, 2, NCH, 3])
        hprev = sb("hprev", [128, 2, NCH])
        hist_c = sb("hist_c", [128, 2, NCH, 30])
        hs_stage = sb("hs_stage", [128, 2, NCH, 16])
        hp_stage = sb("hp_stage", [128, 2, NCH])
        ftmp = sb("ftmp", [128, 16])
        prod = sb("prod", [128, 64 * WC])
        saccb = [sb(f"sacc{i}", [128, 64]) for i in range(2)]
        lcs_buf = sb("lcs_buf", [128, 2, NCH, 16, 3])
        lcp_buf = sb("lcp_buf", [128, 2, NCH, 3])
        cc_stage = [sb(f"cc_stage{i}", [128, 16, 30]) for i in range(2)]
        rstd_t = sb("rstd_t", [128, SEGW])
        nmr_t = sb("nmr_t", [128, SEGW])
        ps = es.enter_context(nc.psum_tensor("ps", [128, 8, 512], F32))

        def R(ap):
            return ap.bitcast(F32R)

        class Buf:
            def __init__(self, ap, key):
                self.ap = ap
                self.key = key
                S.gen[key] = S.gen.get(key, 0) + 1
                self.gen = S.gen[key]

        def wk():
            i = S.wk_i % NWK
            S.wk_i += 1
            return Buf(wkb[i][:], f"wk:{i}")

        def qbuf():
            i = S.q_i % NQ
            S.q_i += 1
            return Buf(qb[i][:], f"q:{i}")

        def PE(fn, reads, writes):
            S.op("pe", fn, reads, writes)

        def ACT(fn, reads, writes):
            S.op("act", fn, reads, writes)

        def DVE(fn, reads, writes):
            S.op("dve", fn, reads, writes)

        def POOL(fn, reads, writes):
            S.op("pool", fn, reads, writes)

        def DMA(fn, reads, writes, sem):
            S.op("sp", fn, reads, writes, sem=sem, inc=16)

        def cs_of(t):
            return slice(t * TW, (t + 1) * TW)

        DMA(lambda e: e.dma_start(out=pvl[:], in_=pvl_d), [], ["pvl"], "d_c0")
        DMA(lambda e: e.dma_start(out=pvc[:], in_=pvc_d), [], ["pvc"], "d_c1")
        DMA(lambda e: e.dma_start(out=pm[:], in_=pm_d), [], ["pm"], "d_c2")
        DMA(lambda e: e.dma_start(out=ident[:], in_=ident_d), [], ["ident"], "d_c3")
        DMA(lambda e: e.dma_start(out=h0s[:], in_=st_lh), [], ["h0s"], "d_c4")
        for jj in range(2):
            DMA(lambda e, jj=jj: e.dma_start(out=lcs_buf[:, jj], in_=st_lc[jj]), [], ["lcs"], f"d_c{5 + jj}")
        POOL(lambda e: e.memset(ones_f[:], 1.0), [], ["ones_f"])
        ACT(lambda e: e.activation(out=R(ones[:]), in_=ones_f[:], func=AF.Copy), ["ones_f"], ["ones"])
        POOL(lambda e: e.memset(hist_l[:], 0.0), [], ["hist_l"])
        POOL(lambda e: e.memset(hist_c[:], 0.0), [], ["hist_c"])
        POOL(lambda e: e.memset(hprev[:], 0.0), [], ["hprev"])
        POOL(lambda e: e.tensor_scalar(out=h0s[:], in0=h0s[:], scalar1=0.5, scalar2=None, op0=ALU.mult), ["h0s"], ["h0s"])
        ACT(lambda e: e.activation(out=dtmp[:], in_=pvl[:, :, :, 7], func=AF.Exp, scale=-1.0), ["pvl"], ["dtmp"])
        ACT(lambda e: e.activation(out=dtmp[:], in_=dtmp[:], func=AF.Ln, bias=1.0, scale=1.0), ["dtmp"], ["dtmp"])
        DVE(lambda e: e.tensor_scalar(out=der[:, :, :, 0], in0=dtmp[:], scalar1=-8.0, scalar2=None, op0=ALU.mult), ["dtmp"], ["der"])
        DVE(lambda e: e.tensor_scalar(out=der[:, :, :, 1], in0=dtmp[:], scalar1=-4.0, scalar2=None, op0=ALU.mult), ["dtmp"], ["der"])
        DVE(lambda e: e.tensor_scalar(out=der[:, :, :, 2], in0=pvl[:, :, :, 5], scalar1=0.5, scalar2=None, op0=ALU.mult), ["pvl"], ["der"])
        DVE(lambda e: e.tensor_scalar(out=der[:, :, :, 3], in0=pvl[:, :, :, 6], scalar1=0.5, scalar2=None, op0=ALU.mult), ["pvl"], ["der"])

        def pieces(seg, t):
            if seg == 2 and t == 1:
                return [("p", 0, P2T1), ("s", P2T1, 64)]
            return [("p", 0, TW)]

        def mm_inproj(b, slot, base, cs):
            def fn(e):
                ins = None
                for kc in range(KC):
                    ins = e.matmul(ps[:, b, 0:TW], R(win[slot][:, base + kc * 128: base + (kc + 1) * 128]),
                                   R(hn_sb[:, kc, cs]), start=(kc == 0), stop=(kc == KC - 1))
                return ins
            return fn

        def emit_inproj(c, b, slot, base, cs, t):
            if c == 0:
                for kc in range(KC):
                    PE(lambda e, kc=kc: e.matmul(ps[:, b, 0:TW], R(win[slot][:, base + kc * 128: base + (kc + 1) * 128]),
                                                 R(hn_sb[:, kc, cs]), start=(kc == 0), stop=(kc == KC - 1)),
                       [f"hn:{t}:{kc}", f"win:{slot}"], [f"ps:{b}"])
            else:
                PE(mm_inproj(b, slot, base, cs), [f"hn:{t}", f"win:{slot}"], [f"ps:{b}"])

        def mm_conv(b, seg, t, W, par, dk0):
            def fn(e):
                ins = None
                for (kind, off, n) in pieces(seg, t):
                    if kind == "s":
                        continue
                    for k in range(W):
                        if kind == "p":
                            rhs = R(strip[par][:, t * TW + off + k: t * TW + off + k + n])
                            out = ps[:, b, off:off + n]
                        else:
                            rhs = R(sstrip[par][:, :, k:k + 4])
                            out = ps[:, b, off:off + n].rearrange("p (s t) -> p s t", t=4)
                        ins = e.matmul(out, R(diag[:, dk0 + k, :]), rhs, start=(k == 0), stop=(k == W - 1))
                return ins
            return fn

        def conv_reads(par, seg, t):
            r = [f"strip{par}:0"]
            r.append(f"strip{par}:h" if t == 0 else f"strip{par}:1")
            return r

        def sample_conv(par, W, wcols, defer=False):
            pv = prod[:, 0:64 * W].rearrange("p (s t k) -> p s t k", s=16, t=4, k=W)
            base = sstrip[par][:, :, 0:W]
            win_ap = bass.AP(base.tensor, base.offset, list(base.ap[:-1]) + [[1, 4], [1, W]])
            wb = wcols.unsqueeze(1).unsqueeze(1).to_broadcast([128, 16, 4, W])
            DVE(lambda e: e.tensor_tensor(out=pv, in0=win_ap, in1=wb, op=ALU.mult), [f"ss{par}:o", f"ss{par}:n", "pvl", "pvc"], ["prod"])
            acc = Buf(saccb[par][:], f"sacc:{par}")

            def emit_reduce():
                DVE(lambda e, acc=acc: e.tensor_reduce(out=acc.ap[:, 0:64], in_=prod[:, 0:64 * W].rearrange("p (n k) -> p n k", k=W),
                                                       axis=mybir.AxisListType.X, op=ALU.add), ["prod"], [acc])
            if defer:
                return acc, emit_reduce
            emit_reduce()
            return acc

        def evac_strip(b, seg, t, W, par, fn_piece, extra_reads):
            for (kind, off, n) in pieces(seg, t):
                if kind == "p":
                    out = R(strip[par][:, W - 1 + t * TW + off: W - 1 + t * TW + off + n])
                    inp = ps[:, b, off:off + n]
                    wkey = f"strip{par}:{t}"
                else:
                    out = R(sstrip[par][:, :, W - 1:W - 1 + 4])
                    inp = ps[:, b, off:off + n].rearrange("p (s t) -> p s t", t=4)
                    wkey = f"ss{par}:n"
                fn_piece(out, inp, off, n, kind, wkey)

        def hist_in(seg, par, W, hist_ap, hkey):
            POOL(lambda e: e.tensor_copy(out=R(strip[par][:, 0:W - 1]), in_=hist_ap), [hkey], [f"strip{par}:h"])

        def hist_out(seg, par, W, hist_ap, hkey):
            if seg < 2:
                POOL(lambda e: e.tensor_copy(out=hist_ap, in_=strip[par][:, SEGW:SEGW + W - 1]), [f"strip{par}:1"], [hkey])

        def phase_norm(gidx, final, seg):
            for t in range(2):
                cs = cs_of(t)
                ACT(lambda e, cs=cs: e.activation(out=R(hn_sb[:, :, cs]), in_=x_sb[:, :, cs], func=AF.Square),
                    [f"x:{t}"], [f"hn:{t}"])
                b = S.bank()

                def mm(e, cs=cs, b=b):
                    ins = None
                    for kc in range(KC):
                        ins = e.matmul(ps[:, b, 0:TW], R(ones[:]), R(hn_sb[:, kc, cs]), start=(kc == 0), stop=(kc == KC - 1))
                    return ins
                PE(mm, [f"hn:{t}", "ones"], [f"ps:{b}"])
                sd = wk()
                ACT(lambda e, b=b, sd=sd: e.activation(out=sd.ap, in_=ps[:, b, 0:TW], func=AF.Sqrt, bias=1e-6, scale=1.0 / 1024.0),
                    [f"ps:{b}"], [sd])
                rr = wk()
                DVE(lambda e, sd=sd, rr=rr: e.reciprocal(out=rr.ap, in_=sd.ap), [sd], [rr])
                for kc in range(KC):
                    outap = x_sb[:, kc, cs] if final else R(hn_sb[:, kc, cs])
                    wkey = f"x:{t}" if final else f"hn:{t}"
                    DVE(lambda e, kc=kc, cs=cs, rr=rr, outap=outap: e.scalar_tensor_tensor(
                        out=outap, in0=x_sb[:, kc, cs], scalar=pm[:, kc, gidx:gidx + 1], in1=rr.ap,
                        op0=ALU.mult, op1=ALU.mult), [f"x:{t}", rr, "pm"], [wkey] + ([] if final else [f"hn:{t}:{kc}"]))
                if final:
                    c0 = seg * SEGW + t * TW
                    DMA(lambda e, cs=cs, c0=c0: e.dma_start(out=y_d[:, :, c0:c0 + TW], in_=x_sb[:, :, cs]),
                        [f"x:{t}"], [], f"d_y{t}")

        win_slot = {"i": 0}

        def load_w(dram_ap, ncols):
            slot = win_slot["i"] % NWIN
            win_slot["i"] += 1
            DMA(lambda e: e.dma_start(out=R(win[slot][:, 0:ncols]), in_=dram_ap), [], [f"win:{slot}"], f"d_win{slot}")
            return slot

        def run_pipeline(stages, pre, n=NCH, ahead=2, first=()):
            for c in range(min(ahead, n)):
                pre(c)
            ns = len(stages)
            for i in range(n + ns - 1):
                for (fn, lag) in first:
                    if 0 <= i - lag < n:
                        fn(i - lag)
                for k, st in enumerate(stages):
                    c = i - k
                    if 0 <= c < n:
                        st(c)
                if i + ahead < n:
                    pre(i + ahead)

        def lru_layer(seg, j):
            ctx = {}

            def pre(c):
                ctx[c] = {"slot": load_w(w_lru[j, c], WINSZ), "xc": [None, None], "q": [None, None]}

            def P1(c):
                slot = ctx[c]["slot"]
                par = c % 2
                dk0 = (c % 2) * WL
                hist_in(seg, par, WL, hist_l[:, j, c, :], "hist_l")
                POOL(lambda e: e.tensor_tensor(out=R(diag[:, dk0:dk0 + WL, :]),
                                               in0=ident[:].unsqueeze(1).to_broadcast([128, WL, 128]),
                                               in1=pvl[:, j, c, 0:WL].unsqueeze(2).to_broadcast([128, WL, 128]),
                                               op=ALU.mult), ["ident", "pvl"], [f"diag:{c % 2}"])
                if seg == 2:
                    POOL(lambda e: e.tensor_copy(out=R(sstrip[par][:, :, 0:WL - 1]), in_=lcs_buf[:, j, c]), ["lcs"], [f"ss{par}:o"])
                for t in range(2):
                    cs = cs_of(t)
                    b = S.bank()
                    emit_inproj(c, b, slot, 0, cs, t)

                    def piece(out, inp, off, n, kind, wkey, b=b):
                        DVE(lambda e: e.tensor_copy(out=out, in_=inp), [f"ps:{b}"], [wkey])
                    evac_strip(b, seg, t, WL, par, piece, [])
                    b2 = S.bank()
                    emit_inproj(c, b2, slot, 1024, cs, t)
                    tz = wk()
                    ACT(lambda e, b2=b2, tz=tz: e.activation(out=tz.ap, in_=ps[:, b2, 0:TW], func=AF.Tanh, scale=0.5),
                        [f"ps:{b2}"], [tz])
                    q = qbuf()
                    DVE(lambda e, b2=b2, tz=tz, q=q: e.scalar_tensor_tensor(out=q.ap, in0=tz.ap, scalar=1.0, in1=ps[:, b2, 0:TW],
                                                                              op0=ALU.add, op1=ALU.mult), [tz, f"ps:{b2}"], [q])
                    ctx[c]["q"][t] = q
                hist_out(seg, par, WL, hist_l[:, j, c, :], "hist_l")
                if seg == 2:
                    ctx[c]["sacc"] = sample_conv(par, WL, pvl[:, j, c, 0:WL])

            def P2(c):
                par = c % 2
                dk0 = (c % 2) * WL
                for t in range(2):
                    b = S.bank()
                    PE(mm_conv(b, seg, t, WL, par, dk0), conv_reads(par, seg, t) + [f"diag:{c % 2}"], [f"ps:{b}"])
                    xc = Buf(xcb[t][:], f"xc:{t}")
                    npc = P2T1 if (seg == 2 and t == 1) else TW
                    ACT(lambda e, b=b, xc=xc, npc=npc: e.activation(out=R(xc.ap[:, 0:npc]), in_=ps[:, b, 0:npc], func=AF.Identity,
                                                                    bias=pvl[:, j, c, 4:5], scale=1.0), [f"ps:{b}", "pvl"], [xc])
                    if npc < TW:
                        sacc = ctx[c]["sacc"]
                        ACT(lambda e, xc=xc, sacc=sacc: e.activation(out=R(xc.ap[:, P2T1:TW]), in_=sacc.ap[:, 0:64], func=AF.Identity,
                                                                     bias=pvl[:, j, c, 4:5], scale=1.0), [sacc, "pvl", xc], [xc])
                    ctx[c]["xc"][t] = xc
                if seg == 2:
                    POOL(lambda e: e.tensor_copy(out=lcs_buf[:, j, c], in_=sstrip[par][:, :, 4:7]), [f"ss{par}:o", f"ss{par}:n"], ["lcs"])
                    POOL(lambda e: e.tensor_copy(out=lcp_buf[:, j, c, :], in_=strip[par][:, NP2:NP2 + 3]), [f"strip{par}:1"], ["lcp"])

            def P3(c):
                slot = ctx[c]["slot"]
                T = [dict() for _ in range(2)]
                for t in range(2):
                    d = T[t]
                    xc = ctx[c]["xc"][t]
                    br = S.bank()
                    PE(lambda e, br=br, xc=xc: e.matmul(ps[:, br, 0:TW], R(win[slot][:, 2048:2176]), R(xc.ap), start=True, stop=True),
                       [xc, f"win:{slot}"], [f"ps:{br}"])
                    bi = S.bank()
                    PE(lambda e, bi=bi, xc=xc: e.matmul(ps[:, bi, 0:TW], R(win[slot][:, 2176:2304]), R(xc.ap), start=True, stop=True),
                       [xc, f"win:{slot}"], [f"ps:{bi}"])
                    tr = wk()
                    ACT(lambda e, br=br, tr=tr: e.activation(out=tr.ap, in_=ps[:, br, 0:TW], func=AF.Tanh, bias=der[:, j, c, 2:3], scale=0.5),
                        [f"ps:{br}", "der"], [tr])
                    ti = wk()
                    ACT(lambda e, bi=bi, ti=ti: e.activation(out=ti.ap, in_=ps[:, bi, 0:TW], func=AF.Tanh, bias=der[:, j, c, 3:4], scale=0.5),
                        [f"ps:{bi}", "der"], [ti])
                    a = wk()
                    ACT(lambda e, tr=tr, a=a: e.activation(out=a.ap, in_=tr.ap, func=AF.Exp, bias=der[:, j, c, 1:2], scale=der[:, j, c, 1:2]),
                        [tr, "der"], [a])
                    m = wk()
                    DVE(lambda e, a=a, m=m: e.scalar_tensor_tensor(out=m.ap, in0=a.ap, scalar=-1.0, in1=a.ap, op0=ALU.mult, op1=ALU.mult), [a], [m])
                    d.update(xc=xc, ti=ti, a=a, m=m)
                for t in range(2):
                    m = T[t]["m"]
                    ACT(lambda e, m=m: e.activation(out=m.ap, in_=m.ap, func=AF.Sqrt, bias=1.0, scale=1.0), [m], [m])
                hprev_buf = None
                for t in range(2):
                    xc, ti = T[t]["xc"], T[t]["ti"]
                    bb = wk()
                    T[t]["bb"] = bb
                    DVE(lambda e, ti=ti, xc=xc, bb=bb: e.scalar_tensor_tensor(out=bb.ap, in0=ti.ap, scalar=1.0, in1=xc.ap, op0=ALU.add, op1=ALU.mult),
                        [ti, xc], [bb])
                for t in range(2):
                    m, bb = T[t]["m"], T[t]["bb"]
                    DVE(lambda e, m=m, bb=bb: e.scalar_tensor_tensor(out=bb.ap, in0=bb.ap, scalar=0.25, in1=m.ap, op0=ALU.mult, op1=ALU.mult),
                        [bb, m], [bb])
                for t in range(2):
                    cs = cs_of(t)
                    xc, ti, a, m, bb = T[t]["xc"], T[t]["ti"], T[t]["a"], T[t]["m"], T[t]["bb"]
                    q = ctx[c]["q"][t]
                    if seg == 2 and t == 1:
                        av = a.ap[:, P2T1:TW].rearrange("p (s t) -> p s t", t=4)
                        bv = bb.ap[:, P2T1:TW].rearrange("p (s t) -> p s t", t=4)
                        DVE(lambda e, av=av: e.tensor_tensor(out=ftmp[:], in0=av[:, :, 0], in1=h0s[:, j, c, :], op=ALU.mult),
                            [a, "h0s"], ["ftmp"])
                        DVE(lambda e, bv=bv: e.tensor_tensor(out=bv[:, :, 0], in0=bv[:, :, 0], in1=ftmp[:], op=ALU.add),
                            [bb, "ftmp"], [bb])
                        DVE(lambda e, av=av: e.memset(av[:, :, 0], 0.0), [a], [a])
                    h = wk()
                    if t == 0:
                        init_ap, init_key = hprev[:, j, c:c + 1], "hprev"
                    else:
                        init_ap, init_key = hprev_buf.ap[:, TW - 1:TW], hprev_buf
                    DVE(lambda e, a=a, bb=bb, h=h, init_ap=init_ap: e.tensor_tensor_scan(out=h.ap, data0=a.ap, data1=bb.ap, initial=init_ap,
                                                                                         op0=ALU.mult, op1=ALU.add),
                        [a, bb, init_key], [h])
                    POOL(lambda e, q=q, h=h, cs=cs: e.tensor_tensor(out=R(stash[:, c, cs]), in0=q.ap, in1=h.ap, op=ALU.mult), [q, h], [f"st:{c}:{t}"])
                    hprev_buf = h
                    if t == 1:
                        if seg < 2:
                            POOL(lambda e, h=h: e.tensor_copy(out=hprev[:, j, c:c + 1], in_=h.ap[:, TW - 1:TW]), [h], ["hprev"])
                        else:
                            POOL(lambda e, h=h: e.tensor_scalar(out=hp_stage[:, j, c:c + 1], in0=h.ap[:, P2T1 - 1:P2T1], scalar1=2.0, scalar2=None, op0=ALU.mult),
                                 [h], ["hp_stage"])
                            hv = h.ap[:, P2T1:TW].rearrange("p (s t) -> p s t", t=4)
                            POOL(lambda e, hv=hv: e.tensor_scalar(out=hs_stage[:, j, c, :], in0=hv[:, :, 3], scalar1=2.0, scalar2=None, op0=ALU.mult),
                                 [h], ["hs_stage"])

            def P23(c):
                P2(c)
                P3(c)

            run_pipeline([P1, P23], pre)

        def ccm_layer(seg, j):
            ctx = {}
            TD = TDS[seg]
            SUMB = [4, 5]
            SQB = [6, 7]

            def pre1(c):
                ctx[c] = {"slot": load_w(w_cvg[j, c], 2048)}

            def P1(c):
                slot = ctx[c]["slot"]
                par = c % 2
                hist_in(seg, par, WC, hist_c[:, j, c, :], "hist_c")
                if seg == 2:
                    DMA(lambda e: e.dma_start(out=cc_stage[par][:], in_=st_cc[j, :, c]), [], [f"ccst{par}"], f"d_ss{par}")
                    POOL(lambda e: e.tensor_copy(out=R(sstrip[par][:, :, 0:WC - 1]), in_=cc_stage[par][:]), [f"ccst{par}"], [f"ss{par}:o"])
                for t in range(2):
                    cs = cs_of(t)
                    bv = S.bank()
                    emit_inproj(c, bv, slot, 0, cs, t)
                    bg = S.bank()
                    emit_inproj(c, bg, slot, 1024, cs, t)
                    sg = wk()
                    ACT(lambda e, bg=bg, sg=sg: e.activation(out=sg.ap, in_=ps[:, bg, 0:TW], func=AF.Sigmoid, bias=pvc[:, j, c, 1:2], scale=1.0),
                        [f"ps:{bg}", "pvc"], [sg])

                    def piece(out, inp, off, n, kind, wkey, bv=bv, sg=sg):
                        sgv = sg.ap[:, off:off + n]
                        if kind == "s":
                            sgv = sgv.rearrange("p (s t) -> p s t", t=4)
                        DVE(lambda e: e.scalar_tensor_tensor(out=out, in0=inp, scalar=pvc[:, j, c, 0:1], in1=sgv, op0=ALU.add, op1=ALU.mult),
                            [f"ps:{bv}", sg, "pvc"], [wkey])
                    evac_strip(bv, seg, t, WC, par, piece, [])
                hist_out(seg, par, WC, hist_c[:, j, c, :], "hist_c")

            NDA = 9 if seg < 2 else 12

            def DG(c):
                for k in range(NDA):
                    ACT(lambda e, k=k: e.activation(out=R(diag[:, k, :]), in_=ident[:], func=AF.Identity, scale=pvc[:, j, c, 3 + k:4 + k]),
                        ["ident", "pvc"], (["diagA"] + DIAG_ALL) if k == 0 else ["diagA"])
                nd = WC - TD - NDA
                DVE(lambda e: e.tensor_tensor(out=R(diag[:, NDA:WC - TD, :]), in0=ident[:].unsqueeze(1).to_broadcast([128, nd, 128]),
                                              in1=pvc[:, j, c, 3 + NDA:3 + WC - TD].unsqueeze(2).to_broadcast([128, nd, 128]), op=ALU.mult),
                    ["ident", "pvc"], DIAG_ALL)

            def TAPS(c):
                par = c % 2
                WPE = WC - TD
                accs = [wk(), wk()]
                xss = [wk() for _ in range(4)]
                xi = 0
                for ki, k in enumerate(range(WPE, WC)):
                    for t in range(2):
                        acc = accs[t]
                        npc = P2T1 if (seg == 2 and t == 1) else TW
                        src = strip[par][:, t * TW + k: t * TW + k + npc]
                        wcol = pvc[:, j, c, 3 + k:4 + k]
                        rd = conv_reads(par, seg, t) + ["pvc"]
                        if ki == 0:
                            ACT(lambda e, src=src, acc=acc, wcol=wcol, npc=npc: e.activation(out=acc.ap[:, 0:npc], in_=src, func=AF.Identity, scale=wcol),
                                rd, [acc])
                        else:
                            xs = xss[xi % 4]
                            xi += 1
                            ACT(lambda e, src=src, xs=xs, wcol=wcol, npc=npc: e.activation(out=xs.ap[:, 0:npc], in_=src, func=AF.Identity, scale=wcol),
                                rd, [xs])
                            DVE(lambda e, xs=xs, acc=acc, npc=npc: e.tensor_tensor(out=acc.ap[:, 0:npc], in0=acc.ap[:, 0:npc], in1=xs.ap[:, 0:npc], op=ALU.add),
                                [xs, acc], [acc])
                ctx[c]["acc"] = accs

            def P2(c):
                par = c % 2
                WPE = WC - TD
                accs = ctx[c].get("acc", [None, None])
                if seg == 2:
                    SC(c)
                for t in range(2):
                    cs = cs_of(t)
                    acc = accs[t]
                    b = S.bank()
                    PE(mm_conv(b, seg, t, WPE, par, 0), conv_reads(par, seg, t) + DIAG_ALL + ["diagA"], [f"ps:{b}"])
                    csq = wk()
                    npc = P2T1 if (seg == 2 and t == 1) else TW
                    c0 = t * TW
                    if True:
                        if TD > 0:
                            DVE(lambda e, b=b, c0=c0, npc=npc, acc=acc: e.scalar_tensor_tensor(out=R(stash[:, c, c0:c0 + npc]), in0=ps[:, b, 0:npc],
                                                                                              scalar=pvc[:, j, c, 34:35], in1=acc.ap[:, 0:npc],
                                                                                              op0=ALU.add, op1=ALU.add),
                                [f"ps:{b}", "pvc", acc], [f"st:{c}:{t}"])
                            ACT(lambda e, c0=c0, npc=npc, csq=csq: e.activation(out=csq.ap[:, 0:npc], in_=stash[:, c, c0:c0 + npc], func=AF.Square),
                                [f"st:{c}:{t}"], [csq])
                        else:
                            ACT(lambda e, b=b, c0=c0, npc=npc: e.activation(out=R(stash[:, c, c0:c0 + npc]), in_=ps[:, b, 0:npc], func=AF.Identity,
                                                                            bias=pvc[:, j, c, 34:35], scale=1.0), [f"ps:{b}", "pvc"], [f"st:{c}:{t}"])
                            ACT(lambda e, b=b, csq=csq, npc=npc: e.activation(out=csq.ap[:, 0:npc], in_=ps[:, b, 0:npc], func=AF.Square,
                                                                              bias=pvc[:, j, c, 34:35], scale=1.0), [f"ps:{b}", "pvc"], [csq])
                        if npc < TW:
                            if ctx[c].get("sacc_emit"):
                                ctx[c].pop("sacc_emit")()
                            sacc = ctx[c]["sacc"]
                            ACT(lambda e, sacc=sacc, c0=c0: e.activation(out=R(stash[:, c, c0 + P2T1:c0 + TW]), in_=sacc.ap[:, 0:64], func=AF.Identity,
                                                                         bias=pvc[:, j, c, 34:35], scale=1.0), [sacc, "pvc", f"st:{c}:{t}"], [f"st:{c}:{t}"])
                            ACT(lambda e, sacc=sacc, csq=csq: e.activation(out=csq.ap[:, P2T1:TW], in_=sacc.ap[:, 0:64], func=AF.Square,
                                                                           bias=pvc[:, j, c, 34:35], scale=1.0), [sacc, "pvc", csq], [csq])
                    if c == 0:
                        POOL(lambda e, cs=cs: e.tensor_copy(out=rstd_t[:, cs], in_=stash[:, c, cs]), [f"st:{c}:{t}"], [f"rstd:{t}"])
                        POOL(lambda e, cs=cs, csq=csq: e.tensor_copy(out=nmr_t[:, cs], in_=csq.ap), [csq], [f"nmr:{t}"])
                    else:
                        POOL(lambda e, cs=cs: e.tensor_tensor(out=rstd_t[:, cs], in0=rstd_t[:, cs], in1=stash[:, c, cs], op=ALU.add),
                             [f"st:{c}:{t}", f"rstd:{t}"], [f"rstd:{t}"])
                        POOL(lambda e, cs=cs, csq=csq: e.tensor_tensor(out=nmr_t[:, cs], in0=nmr_t[:, cs], in1=csq.ap, op=ALU.add),
                             [csq, f"nmr:{t}"], [f"nmr:{t}"])
                if seg == 2:
                    POOL(lambda e: e.tensor_copy(out=cc_stage[par][:], in_=sstrip[par][:, :, 4:34]), [f"ss{par}:o", f"ss{par}:n"], [f"ccst{par}"])
                    DMA(lambda e: e.dma_start(out=o_ccs[j, :, c], in_=cc_stage[par][:]), [f"ccst{par}"], [], f"d_ss{par}")
                    DMA(lambda e: e.dma_start(out=o_ccp[j, :, c, :], in_=strip[par][:, NP2:NP2 + 30]), [f"strip{par}:1"], [], f"d_st{par}")

            def SC(c):
                ctx[c]["sacc"], ctx[c]["sacc_emit"] = sample_conv(c % 2, WC, pvc[:, j, c, 3:3 + WC], defer=True)

            firsts = [(DG, 1)] + ([(TAPS, 1)] if TD > 0 else [])
            run_pipeline([P1, P2], pre1, first=firsts)

            st = [dict() for _ in range(2)]
            for t in range(2):
                cs = cs_of(t)
                s1 = Buf(xcb[0][:], "xc:0")
                ACT(lambda e, cs=cs, s1=s1: e.activation(out=R(s1.ap), in_=rstd_t[:, cs], func=AF.Copy), [f"rstd:{t}"], [s1])
                s2 = Buf(xcb[1][:], "xc:1")
                ACT(lambda e, cs=cs, s2=s2: e.activation(out=R(s2.ap), in_=nmr_t[:, cs], func=AF.Copy), [f"nmr:{t}"], [s2])
                b1 = S.bank()
                PE(lambda e, b1=b1, s1=s1: e.matmul(ps[:, b1, 0:TW], R(ones[:]), R(s1.ap), start=True, stop=True), [s1, "ones"], [f"ps:{b1}"])
                b2 = S.bank()
                PE(lambda e, b2=b2, s2=s2: e.matmul(ps[:, b2, 0:TW], R(ones[:]), R(s2.ap), start=True, stop=True), [s2, "ones"], [f"ps:{b2}"])
                st[t].update(b1=b1, b2=b2)
            for t in range(2):
                mu = wk()
                b1 = st[t]["b1"]
                DVE(lambda e, b1=b1, mu=mu: e.tensor_scalar(out=mu.ap, in0=ps[:, b1, 0:TW], scalar1=1.0 / 2048.0, scalar2=None, op0=ALU.mult),
                    [f"ps:{b1}"], [mu])
                st[t]["mu"] = mu
            for t in range(2):
                mu = st[t]["mu"]
                msq = wk()
                DVE(lambda e, mu=mu, msq=msq: e.tensor_tensor(out=msq.ap, in0=mu.ap, in1=mu.ap, op=ALU.mult), [mu], [msq])
                st[t]["msq"] = msq
            for t in range(2):
                b2, msq = st[t]["b2"], st[t]["msq"]
                var = wk()
                DVE(lambda e, b2=b2, msq=msq, var=var: e.scalar_tensor_tensor(out=var.ap, in0=ps[:, b2, 0:TW], scalar=1.0 / 2048.0, in1=msq.ap,
                                                                             op0=ALU.mult, op1=ALU.subtract), [f"ps:{b2}", msq], [var])
                st[t]["var"] = var
            for t in range(2):
                var = st[t]["var"]
                sd = wk()
                ACT(lambda e, var=var, sd=sd: e.activation(out=sd.ap, in_=var.ap, func=AF.Sqrt, bias=1e-5, scale=1.0), [var], [sd])
                st[t]["sd"] = sd
            for t in range(2):
                cs = cs_of(t)
                sd = st[t]["sd"]
                DVE(lambda e, sd=sd, cs=cs: e.reciprocal(out=rstd_t[:, cs], in_=sd.ap), [sd], [f"rstd:{t}"])
            for t in range(2):
                cs = cs_of(t)
                mu = st[t]["mu"]
                DVE(lambda e, mu=mu, cs=cs: e.scalar_tensor_tensor(out=nmr_t[:, cs], in0=mu.ap, scalar=-1.0, in1=rstd_t[:, cs],
                                                                   op0=ALU.mult, op1=ALU.mult), [mu, f"rstd:{t}"], [f"nmr:{t}"])
            S.nbank = 8

            def pre2(c):
                ctx[c]["slot2"] = load_w(w_cz[j, c], 1024)

            def Q1(c):
                slot = ctx[c]["slot2"]
                ctx[c]["sz"] = [None, None]
                ctx[c]["d"] = [None, None]
                for t in range(2):
                    cs = cs_of(t)
                    bz = S.bank()
                    PE(mm_inproj(bz, slot, 0, cs), [f"hn:{t}", f"win:{slot}"], [f"ps:{bz}"])
                    sz = wk()
                    ACT(lambda e, bz=bz, sz=sz: e.activation(out=sz.ap, in_=ps[:, bz, 0:TW], func=AF.Silu, bias=pvc[:, j, c, 2:3], scale=1.0),
                        [f"ps:{bz}", "pvc"], [sz])
                    d = wk()
                    DVE(lambda e, d=d, cs=cs: e.tensor_tensor(out=d.ap, in0=stash[:, c, cs], in1=rstd_t[:, cs], op=ALU.mult),
                        [f"st:{c}:{t}", f"rstd:{t}"], [d])
                    ctx[c]["sz"][t] = sz
                    ctx[c]["d"][t] = d
                for t in range(2):
                    cs = cs_of(t)
                    d = ctx[c]["d"][t]
                    DVE(lambda e, d=d, cs=cs: e.tensor_tensor(out=d.ap, in0=d.ap, in1=nmr_t[:, cs], op=ALU.add),
                        [d, f"nmr:{t}"], [d])

            def Q2(c):
                for t in range(2):
                    cs = cs_of(t)
                    sz = ctx[c]["sz"][t]
                    d = ctx[c]["d"][t]
                    ACT(lambda e, d=d: e.activation(out=d.ap, in_=d.ap, func=AF.Silu, bias=pvc[:, j, c, 36:37], scale=pvc[:, j, c, 35:36]),
                        [d, "pvc"], [d])
                    POOL(lambda e, d=d, sz=sz, cs=cs: e.tensor_tensor(out=R(stash[:, c, cs]), in0=d.ap, in1=sz.ap, op=ALU.mult),
                         [d, sz], [f"st:{c}:{t}"])

            run_pipeline([Q1, Q2], pre2)

        def phase_out(w_d, j, bias_idx):
            slots = [(wout[0], WOUT_KEYS[0], "d_wout0"), (wout[1], WOUT_KEYS[1], "d_wout1")]
            for i in range(NWIN):
                slots.append((win[i][:, 0:2048], [f"win:{i}"], f"d_win{i}"))
            NS = len(slots)

            def load(m):
                ap, keys, sem = slots[m % NS]
                DMA(lambda e: e.dma_start(out=R(ap), in_=w_d[j, m]), [], keys, sem)
            for m in range(min(NS, 8)):
                load(m)
            for m in range(8):
                ap, keys, sem = slots[m % NS]
                for t in range(2):
                    cs = cs_of(t)
                    b = S.bank()

                    if m == 0:
                        for c in range(NCH):
                            PE(lambda e, b=b, cs=cs, ap=ap, c=c: e.matmul(ps[:, b, 0:TW], R(ap[:, c * 128:(c + 1) * 128]), R(stash[:, c, cs]),
                                                                        start=(c == 0), stop=(c == NCH - 1)),
                               [f"st:{c}:{t}"] + keys, [f"ps:{b}"])
                    else:
                        def mm(e, b=b, cs=cs, ap=ap):
                            ins = None
                            for c in range(NCH):
                                ins = e.matmul(ps[:, b, 0:TW], R(ap[:, c * 128:(c + 1) * 128]), R(stash[:, c, cs]),
                                               start=(c == 0), stop=(c == NCH - 1))
                            return ins
                        PE(mm, [f"st:{c}:{t}" for c in range(NCH)] + keys, [f"ps:{b}"])
                    sc = 0.0 if bias_idx is None else pm[:, m, bias_idx:bias_idx + 1]
                    DVE(lambda e, b=b, cs=cs, m=m, sc=sc: e.scalar_tensor_tensor(out=x_sb[:, m, cs], in0=ps[:, b, 0:TW], scalar=sc, in1=x_sb[:, m, cs],
                                                                                op0=ALU.add, op1=ALU.add), [f"ps:{b}", f"x:{t}", "pm"], [f"x:{t}"])
                if m + NS < 8:
                    load(m + NS)

        for seg in range(NSEG):
            for t in range(2):
                c0 = seg * SEGW + t * TW
                DMA(lambda e, t=t, c0=c0: e.dma_start(out=x_sb[:, :, t * TW:(t + 1) * TW], in_=xin[:, :, c0:c0 + TW]), [], [f"x:{t}"], f"d_x{t}")
            for l in range(4):
                j = l // 2
                phase_norm(l, False, seg)
                if l % 2 == 0:
                    lru_layer(seg, j)
                    phase_out(w_lo, j, None)
                else:
                    ccm_layer(seg, j)
                    phase_out(w_co, j, 5 + j)
            phase_norm(4, True, seg)
        for jj in range(2):
            DMA(lambda e, jj=jj: e.dma_start(out=o_lcs[jj], in_=lcs_buf[:, jj]), ["lcs"], [], f"d_o{2 + jj}")
            DMA(lambda e, jj=jj: e.dma_start(out=o_lcp[jj], in_=lcp_buf[:, jj]), ["lcp"], [], f"d_o{4 + jj}")
        DMA(lambda e: e.dma_start(out=o_lhp, in_=hp_stage[:]), ["hp_stage"], [], "d_o0")
        DMA(lambda e: e.dma_start(out=o_lhs, in_=hs_stage[:]), ["hs_stage"], [], "d_o1")
        final_waits = [(k, v) for k, v in S.cnt.items() if k.startswith("d_")]

        semnames = sorted(S.cnt.keys())
        sems = {k: es.enter_context(nc.semaphore(k)) for k in semnames}
        block = es.enter_context(nc.Block())

        def replay(engname):
            def body(e):
                for ent in S.prog[engname]:
                    if ent[0] == "wait":
                        e.wait_ge(sems[ent[1]], ent[2])
                    else:
                        ins = ent[1](e)
                        ins.then_inc(sems[ent[2]], ent[3])
                if engname == "sp":
                    for k, v in final_waits:
                        e.wait_ge(sems[k], v)
            return body

        block.tensor(replay("pe"))
        block.scalar(replay("act"))
        block.vector(replay("dve"))
        block.gpsimd(replay("pool"))
        block.sync(replay("sp"))
    return nc


_CACHE = {}


def _host_layout(inp, i):
    f = np.float32
    xp = inp["x_prompt"][i]
    xs = inp["x_sample"][16 * i:16 * i + 16].reshape(64, 1024)
    X = np.concatenate([xp.T, xs.T], axis=1)
    xin = np.ascontiguousarray(X.reshape(KC, 128, TALL).transpose(1, 0, 2), dtype=f)
    slc = inp["state_lru_conv"][:, 16 * i:16 * i + 16]
    st_lc = np.ascontiguousarray(slc.reshape(2, 16, 3, NCH, 128).transpose(0, 4, 3, 1, 2), dtype=f)
    slh = inp["state_lru_h"][:, 16 * i:16 * i + 16]
    st_lh = np.ascontiguousarray(slh.reshape(2, 16, NCH, 128).transpose(3, 0, 2, 1), dtype=f)
    scc = inp["state_ccm_conv"][:, 16 * i:16 * i + 16]
    st_cc = np.ascontiguousarray(scc.reshape(2, 16, 30, NCH, 128).transpose(0, 4, 3, 1, 2), dtype=f)
    return {"xin": xin, "st_lc": st_lc, "st_lh": st_lh, "st_cc": st_cc}


def _host_weights(inp):
    f = np.float32
    wl = inp["lru_w_in"].reshape(2, KC, 128, 2, NCH, 128).transpose(0, 4, 2, 3, 1, 5).reshape(2, NCH, 128, 2048)
    w_lru = np.ascontiguousarray(np.concatenate([wl, inp["lru_w_a"], inp["lru_w_i"]], axis=-1), dtype=f)
    wc = inp["ccm_w_in"].reshape(2, KC, 128, 3, NCH, 128).transpose(0, 4, 2, 3, 1, 5)
    w_cvg = np.ascontiguousarray(wc[:, :, :, 0:2].reshape(2, NCH, 128, 2048), dtype=f)
    w_cz = np.ascontiguousarray(wc[:, :, :, 2].reshape(2, NCH, 128, 1024), dtype=f)
    w_lo = np.ascontiguousarray(inp["lru_w_out"].reshape(2, NCH, 128, 8, 128).transpose(0, 3, 2, 1, 4).reshape(2, 8, 128, 2048), dtype=f)
    w_co = np.ascontiguousarray(inp["ccm_w_out"].reshape(2, NCH, 128, 8, 128).transpose(0, 3, 2, 1, 4).reshape(2, 8, 128, 2048), dtype=f)
    pvl = np.concatenate([inp["lru_conv_w"].transpose(0, 2, 1), inp["lru_conv_b"][..., None], inp["lru_b_a"][..., None],
                          inp["lru_b_i"][..., None], inp["lru_lam"][..., None]], axis=-1)
    pv_lru = np.ascontiguousarray(pvl.reshape(2, NCH, 128, 8).transpose(2, 0, 1, 3), dtype=f)
    pvc = np.concatenate([inp["ccm_b_in"].reshape(2, 3, 2048).transpose(0, 2, 1), inp["ccm_dw_w"].transpose(0, 2, 1),
                          inp["ccm_dw_b"][..., None], inp["ccm_ln_g"][..., None], inp["ccm_ln_b"][..., None]], axis=-1)
    pv_ccm = np.ascontiguousarray(pvc.reshape(2, NCH, 128, 37).transpose(2, 0, 1, 3), dtype=f)
    pmv = np.concatenate([inp["norm_g"].T, inp["final_norm_g"][:, None], inp["ccm_b_out"].T], axis=-1)
    pm = np.ascontiguousarray(pmv.reshape(KC, 128, 7).transpose(1, 0, 2), dtype=f)
    ident = np.eye(128, dtype=f)
    return {"w_lru": w_lru, "w_cvg": w_cvg, "w_cz": w_cz, "w_lo": w_lo, "w_co": w_co,
            "pv_lru": pv_lru, "pv_ccm": pv_ccm, "pm": pm, "ident": ident}


def kernel(**inputs):
    inp = {k: np.asarray(v) for k, v in inputs.items()}
    if "nc" not in _CACHE:
        _CACHE["nc"] = build_program()
    nc = _CACHE["nc"]
    wts = _host_weights(inp)
    in_maps = []
    for i in range(NCORES):
        m = dict(wts)
        m.update(_host_layout(inp, i))
        in_maps.append(m)
    res = run_bass_kernel_spmd(nc, in_maps, core_ids=list(range(NCORES)))
    f = np.float32
    y_prompt = np.zeros((8, 2048, 1024), f)
    y_sample = np.zeros((128, 4, 1024), f)
    lru_conv_p = np.zeros((2, 8, 3, 2048), f)
    lru_h_p = np.zeros((2, 8, 2048), f)
    ccm_conv_p = np.zeros((2, 8, 30, 2048), f)
    lru_conv_s = np.zeros((2, 128, 3, 2048), f)
    lru_h_s = np.zeros((2, 128, 2048), f)
    ccm_conv_s = np.zeros((2, 128, 30, 2048), f)
    for i in range(NCORES):
        r = res.results[i]
        Y = np.asarray(r["y"]).transpose(1, 0, 2).reshape(1024, TALL)
        y_prompt[i] = Y[:, :2048].T
        y_sample[16 * i:16 * i + 16] = Y[:, 2048:].T.reshape(16, 4, 1024)
        lru_conv_p[:, i] = np.asarray(r["o_lcp"]).transpose(0, 3, 2, 1).reshape(2, 3, 2048)
        lru_h_p[:, i] = np.asarray(r["o_lhp"]).transpose(1, 2, 0).reshape(2, 2048)
        ccm_conv_p[:, i] = np.asarray(r["o_ccp"]).transpose(0, 3, 2, 1).reshape(2, 30, 2048)
        lru_conv_s[:, 16 * i:16 * i + 16] = np.asarray(r["o_lcs"]).transpose(0, 3, 4, 2, 1).reshape(2, 16, 3, 2048)
        lru_h_s[:, 16 * i:16 * i + 16] = np.asarray(r["o_lhs"]).transpose(1, 3, 2, 0).reshape(2, 16, 2048)
        ccm_conv_s[:, 16 * i:16 * i + 16] = np.asarray(r["o_ccs"]).transpose(0, 3, 4, 2, 1).reshape(2, 16, 30, 2048)
    return (y_prompt, y_sample, lru_conv_p, lru_h_p, ccm_conv_p, lru_conv_s, lru_h_s, ccm_conv_s)
```
